# Optimizing a Trainium2 kernel written in Bass

```python
import math
import jax, jax.numpy as jnp
from jax import lax
import numpy as np

D_MODEL = 1024
BATCH = 16
SEQ = 4096
DEPTH = 2

CONV_WIDTH = D_MODEL // 2
CONV_KERNEL = 31
N_HEADS = 8
HEAD_DIM = 64
ATT_WIDTH = N_HEADS * HEAD_DIM
N_BRANCH = 2
D_FF = -(-8 * D_MODEL // (3 * 256)) * 256
IN_COLS = 2 * CONV_WIDTH + 3 * ATT_WIDTH + N_BRANCH * D_MODEL
Q_BLOCK = 128
EPS = 1e-6

kernel_name = "hybrid_conformer_stickbreaking_block"


def rmsnorm(x, g):
    xf = x.astype(jnp.float32)
    y = xf * lax.rsqrt(jnp.mean(xf * xf, axis=-1, keepdims=True) + EPS)
    return (y * g.astype(jnp.float32)).astype(x.dtype)


def layernorm(x, g, b):
    xf = x.astype(jnp.float32)
    mu = jnp.mean(xf, axis=-1, keepdims=True)
    xc = xf - mu
    y = xc * lax.rsqrt(jnp.mean(xc * xc, axis=-1, keepdims=True) + EPS)
    return (y * g.astype(jnp.float32) + b.astype(jnp.float32)).astype(x.dtype)


def causal_depthwise_conv(u, w, b):
    ch = u.shape[-1]
    y = lax.conv_general_dilated(
        u, w[:, None, :].astype(u.dtype), window_strides=(1,), padding=[(CONV_KERNEL - 1, 0)],
        dimension_numbers=("NWC", "WIO", "NWC"), feature_group_count=ch)
    return y + b


def stick_breaking_attention(q, k, v):
    s_len = q.shape[1]
    scale = 1.0 / math.sqrt(q.shape[-1])
    qf = q.astype(jnp.float32) * scale
    kf = k.astype(jnp.float32)
    vf = v.astype(jnp.float32)
    outs = []
    for start in range(0, s_len, Q_BLOCK):
        end = start + Q_BLOCK
        z = jnp.einsum("bqhd,bkhd->bhqk", qf[:, start:end], kf[:, :end])
        t_idx = start + jnp.arange(Q_BLOCK)[:, None]
        s_idx = jnp.arange(end)[None, :]
        causal = s_idx < t_idx
        log_keep = jnp.where(causal, jax.nn.log_sigmoid(-z), 0.0)
        log_after = lax.cumsum(log_keep, axis=3, reverse=True) - log_keep
        a = jnp.where(causal, jnp.exp(jax.nn.log_sigmoid(z) + log_after), 0.0)
        outs.append(jnp.einsum("bhqk,bkhd->bqhd", a, vf[:, :end]))
    return jnp.concatenate(outs, axis=1).astype(q.dtype)


def hybrid_mixer(h, w_in, conv_w, conv_b, conv_ln_g, conv_ln_b, w_conv_out, w_att_out, w_o):
    b, s, _ = h.shape
    proj = h @ w_in
    glu_in, qkv, gates = jnp.split(proj, [2 * CONV_WIDTH, 2 * CONV_WIDTH + 3 * ATT_WIDTH], axis=-1)
    u_val, u_gate = jnp.split(glu_in, 2, axis=-1)
    u = u_val * jax.nn.sigmoid(u_gate)
    u = causal_depthwise_conv(u, conv_w, conv_b)
    u = jax.nn.silu(layernorm(u, conv_ln_g, conv_ln_b))
    y_conv = u @ w_conv_out
    qkv = qkv.reshape(b, s, 3, N_HEADS, HEAD_DIM)
    o = stick_breaking_attention(qkv[:, :, 0], qkv[:, :, 1], qkv[:, :, 2])
    y_att = o.reshape(b, s, ATT_WIDTH) @ w_att_out
    g_conv, g_att = jnp.split(gates, 2, axis=-1)
    merged = jax.nn.sigmoid(g_conv) * y_conv + jax.nn.sigmoid(g_att) * y_att
    return merged @ w_o


def swiglu_ffn(h, w_ffn_in, w_ffn_out):
    gate, up = jnp.split(h @ w_ffn_in, 2, axis=-1)
    return (jax.nn.silu(gate) * up) @ w_ffn_out


def _normal(k, shape, scale):
    return jax.random.normal(k, shape, jnp.float32) * scale


def setup_inputs(seed: int = 0) -> dict:
    key = jax.random.key(seed)
    ks = jax.random.split(key, 20)
    D = D_MODEL
    return {
        "x": _normal(ks[0], (BATCH, SEQ, D), 1.0),
        "c": _normal(ks[1], (BATCH, D), 1.0),
        "ada_w": _normal(ks[2], (DEPTH, D, 6 * D), D ** -0.5),
        "ada_b": _normal(ks[3], (DEPTH, 6 * D), 0.02),
        "pre_mix_g": 1.0 + _normal(ks[4], (DEPTH, D), 0.02),
        "post_mix_g": 1.0 + _normal(ks[5], (DEPTH, D), 0.02),
        "pre_ffn_g": 1.0 + _normal(ks[6], (DEPTH, D), 0.02),
        "post_ffn_g": 1.0 + _normal(ks[7], (DEPTH, D), 0.02),
        "w_in": _normal(ks[8], (DEPTH, D, IN_COLS), D ** -0.5),
        "conv_w": _normal(ks[9], (DEPTH, CONV_KERNEL, CONV_WIDTH), CONV_KERNEL ** -0.5),
        "conv_b": _normal(ks[10], (DEPTH, CONV_WIDTH), 0.02),
        "conv_ln_g": 1.0 + _normal(ks[11], (DEPTH, CONV_WIDTH), 0.02),
        "conv_ln_b": _normal(ks[12], (DEPTH, CONV_WIDTH), 0.02),
        "w_conv_out": _normal(ks[13], (DEPTH, CONV_WIDTH, D), CONV_WIDTH ** -0.5),
        "w_att_out": _normal(ks[14], (DEPTH, ATT_WIDTH, D), ATT_WIDTH ** -0.5),
        "w_o": _normal(ks[15], (DEPTH, D, D), D ** -0.5),
        "w_ffn_in": _normal(ks[16], (DEPTH, D, 2 * D_FF), D ** -0.5),
        "w_ffn_out": _normal(ks[17], (DEPTH, D_FF, D), D_FF ** -0.5),
    }


def reference(x, c, ada_w, ada_b, pre_mix_g, post_mix_g, pre_ffn_g, post_ffn_g, w_in, conv_w, conv_b,
              conv_ln_g, conv_ln_b, w_conv_out, w_att_out, w_o, w_ffn_in, w_ffn_out):
    c_act = jax.nn.silu(c)
    for l in range(DEPTH):
        mod = c_act @ ada_w[l] + ada_b[l]
        sh1, sc1, ga1, sh2, sc2, ga2 = [m[:, None, :] for m in jnp.split(mod, 6, axis=-1)]
        h = rmsnorm(x, pre_mix_g[l]) * (1.0 + sc1) + sh1
        y = hybrid_mixer(h, w_in[l], conv_w[l], conv_b[l], conv_ln_g[l], conv_ln_b[l],
                         w_conv_out[l], w_att_out[l], w_o[l])
        x = x + ga1 * rmsnorm(y, post_mix_g[l])
        h = rmsnorm(x, pre_ffn_g[l]) * (1.0 + sc2) + sh2
        y = swiglu_ffn(h, w_ffn_in[l], w_ffn_out[l])
        x = x + ga2 * rmsnorm(y, post_ffn_g[l])
    return x
```

```python
import contextlib
import numpy as np
import concourse.bass as bass
import concourse.mybir as mybir
from concourse.bass_utils import run_bass_kernel_spmd

F32 = mybir.dt.float32
BF16 = mybir.dt.bfloat16
AF = mybir.ActivationFunctionType
ALU = mybir.AluOpType

D = 1024
NCH = 8
T = 512
CW = 512
NH = 8
HD = 64
DFF = 2816
NHID = 22
KCONV = 31
HALO = KCONV - 1
INC = 4608
EPS = 1e-6
NSLOT = 4
SLOT = 4096
NCORES = 8

TILES = ([("glu", c) for c in range(4)] + [("q", 0), ("k", 0), ("v", 0)] +
         [("mix", c) for c in range(8)] + [("wo", t) for t in range(2)] +
         [("fin", i) for i in range(11)] + [("fout", c) for c in range(8)])
NT = len(TILES)

SP_PRE1, SP_POST1, SP_PRE2, SP_POST2 = 0, 8, 16, 24
SP_ADAB = 32
SP_CONVB = 80
SP_LNG = 84
SP_LNB = 88
SP_CONVW = 92
NSP = 92 + KCONV * 4

CF_IDENT = 0
CF_ONES = 128
CF_MEAN = 256
CF_MASK = 384
CF_TRI1 = 512
CF_TRI2 = 640
CF_EPS = 768
NCF = 772


class Buf:
    __slots__ = ("name", "w", "r", "semkey", "dcnt")

    def __init__(self, name):
        self.name = name
        self.w = None
        self.r = []
        self.semkey = None
        self.dcnt = 0


class Queue:
    def __init__(self, name, semkey, is_pe=False):
        self.name = name
        self.semkey = semkey
        self.cnt = 0
        self.ops = []
        self.seen = {}
        self.is_pe = is_pe


class Prog:
    def __init__(self, nc, stack):
        self.nc = nc
        self.stack = stack
        self.sems = []
        self.dma_sem_cnt = {}
        self.queues = {}
        for name in ("pe", "act", "dve", "pool", "sp"):
            k = self.new_sem(name)
            self.queues[name] = Queue(name, k, is_pe=(name == "pe"))

    def new_sem(self, name):
        s = self.stack.enter_context(self.nc.semaphore("s_" + name))
        self.sems.append(s)
        return len(self.sems) - 1

    def op(self, qname, fn, reads=(), writes=(), dma=None):
        q = self.queues[qname]
        need = {}

        def add(ev):
            if ev is None:
                return
            k, v = ev
            if need.get(k, 0) < v:
                need[k] = v

        for b in reads:
            add(b.w)
        for b in writes:
            add(b.w)
            for ev in b.r:
                add(ev)
        waits = []
        for k, v in need.items():
            if q.is_pe and k == q.semkey:
                continue
            if k in self.dma_sem_cnt:
                v = max(v, self.dma_sem_cnt[k])
            if q.seen.get(k, 0) >= v:
                continue
            q.seen[k] = v
            waits.append((k, v))
        if dma is not None:
            if dma.semkey is None:
                dma.semkey = {}
            if qname not in dma.semkey:
                dma.semkey[qname] = self.new_sem("d_" + dma.name + "_" + qname)
                self.dma_sem_cnt[dma.semkey[qname]] = 0
            semkey = dma.semkey[qname]
        else:
            semkey = q.semkey
        rec = {"waits": waits, "fn": fn, "semkey": semkey, "dma": dma is not None}
        q.ops.append(rec)
        if dma is not None:
            n = getattr(fn, "n", 1)
            self.dma_sem_cnt[semkey] += 16 * n
            ev = (semkey, self.dma_sem_cnt[semkey])
        else:
            q.cnt += 1
            ev = (semkey, q.cnt)
        for b in writes:
            b.w = ev
            b.r = []
        for b in reads:
            b.r.append(ev)
        return ev

    def final_wait(self, qname, bufs):
        q = self.queues[qname]
        need = {}
        for b in bufs:
            for ev in [b.w] + list(b.r):
                if ev is None:
                    continue
                k, v = ev
                if k in self.dma_sem_cnt:
                    v = max(v, self.dma_sem_cnt[k])
                need[k] = max(need.get(k, 0), v)
        q.ops.append({"waits": list(need.items()), "fn": None, "semkey": None, "dma": False})

    def emit(self, qname, eng):
        q = self.queues[qname]
        for rec in q.ops:
            for k, v in rec["waits"]:
                eng.wait_ge(self.sems[k], v)
            if rec["fn"] is None:
                continue
            insts = rec["fn"](eng)
            sem = self.sems[rec["semkey"]]
            if rec["dma"]:
                for ins in insts:
                    ins.then_inc(sem, 16)
            else:
                insts[-1].then_inc(sem, 1)


def build(NSEQ, S, DEPTH, STOP=99):
    NG = S // T
    NKB = S // 128
    nc = bass.Bass("TRN2", target_bir_lowering=False)

    def din(name, shape, dt=F32):
        return nc.dram_tensor(name, list(shape), dt, kind="ExternalInput").ap()

    x_d = din("x", [NSEQ, S, D])
    cT_d = din("cT", [128, NCH, NSEQ])
    ada_w_d = din("ada_w", [DEPTH, D, 6 * D])
    smallp_d = din("smallp", [DEPTH, 128, NSP])
    consts_d = din("consts", [128, NCF])
    w_in_d = din("w_in", [DEPTH, D, INC])
    w_conv_out_d = din("w_conv_out", [DEPTH, CW, D])
    w_att_out_d = din("w_att_out", [DEPTH, CW, D])
    w_o_d = din("w_o", [DEPTH, D, D])
    w_ffn_in_d = din("w_ffn_in", [DEPTH, D, 2 * DFF])
    w_ffn_out_d = din("w_ffn_out", [DEPTH, DFF, D])
    out_d = nc.dram_tensor("out", [NSEQ, S, D], F32, kind="ExternalOutput").ap()
    wbf_d = nc.dram_tensor("wbf", [DEPTH, NT, 128, SLOT], BF16, kind="Internal").ap()
    x1T_d = nc.dram_tensor("x1T", [NSEQ, NG, 128, NCH * T], F32, kind="Internal").ap()

    with contextlib.ExitStack() as stack:
        def sb(name, shape, dt):
            return stack.enter_context(nc.sbuf_tensor(name, list(shape), dt))

        def ps(name):
            return stack.enter_context(nc.psum_tensor(name, [128, 512], F32))

        kT_t = sb("kT", [128, 4, S], BF16)
        V_t = sb("Vc", [128, NKB, 512], BF16)
        hT_t = sb("hT", [128, NCH, T], BF16)
        ring_t = [sb("ring%d" % i, [128, SLOT], BF16) for i in range(NSLOT)]
        R1_t = sb("R1", [128, NHID * T], BF16)
        uT_t = sb("uT", [128, 4, HALO + T], BF16)
        sp_t = [sb("sp%d" % i, [128, T], BF16) for i in range(3)]
        A_t = [sb("A%d" % i, [128, T], BF16) for i in range(2)]
        sq_t = [sb("sq%d" % i, [128, T], BF16) for i in range(2)]
        cbf_t = sb("cbf", [128, 512], BF16)
        xT_t = sb("xT", [128, NCH * T], F32)
        y_t = sb("y", [128, NCH * T], F32)
        e_t = [sb("e%d" % i, [128, T], F32) for i in range(4)]
        ea_t = [sb("ea%d" % i, [128, T], F32) for i in range(2)]
        rstd_t = sb("rstd", [128, T], F32)
        tmp_t = [sb("tmp%d" % i, [128, T], F32) for i in range(3)]
        sig_t = [sb("sig%d" % i, [128, T], F32) for i in range(2)]
        ptmp_t = sb("ptmp", [128, T], F32)
        cf_t = sb("cf", [128, NCF], F32)
        spm_t = sb("spm", [128, NSP], F32)
        cT_t = sb("cTs", [128, NCH, NSEQ], F32)
        cact_t = sb("cact", [128, NCH, NSEQ], F32)
        mod_t = sb("mod", [128, 48, NSEQ], F32)
        dsc_t = sb("dsc", [128, 4, NSEQ, NCH], F32)
        bank_t = [ps("bank%d" % i) for i in range(8)]

        P = Prog(nc, stack)

        B = {}
        for nm in ["kT", "V", "hT", "R1", "uT", "cbf", "xT", "y", "rstd", "ptmp", "cf", "spm",
                   "cT", "cact", "mod", "dsc", "x_in", "out", "ada_w", "smallp", "consts",
                   "wsrc", "cT_d"]:
            B[nm] = Buf(nm)
        Bring = [Buf("ring%d" % i) for i in range(NSLOT)]
        Bsp = [Buf("sp%d" % i) for i in range(3)]
        BA = [Buf("A%d" % i) for i in range(2)]
        Bsq = [Buf("sq%d" % i) for i in range(2)]
        Be = [Buf("e%d" % i) for i in range(4)]
        Bea = [Buf("ea%d" % i) for i in range(2)]
        Btmp = [Buf("tmp%d" % i) for i in range(3)]
        Bsig = [Buf("sig%d" % i) for i in range(2)]
        Bbank = [Buf("bank%d" % i) for i in range(8)]
        Bwbf = [[Buf("wbf%d_%d" % (l, t)) for t in range(NT)] for l in range(DEPTH)]
        Bx1 = [[Buf("x1_%d_%d" % (s, g)) for g in range(NG)] for s in range(NSEQ)]

        xT = xT_t[:].rearrange("p (c t) -> p c t", c=NCH)
        y3 = y_t[:].rearrange("p (c t) -> p c t", c=NCH)
        ystage = y_t[:].rearrange("p (b d) -> p b d", b=4)
        vconv = y3
        qT = R1_t[:, 0:4 * T].rearrange("p (c t) -> p c t", c=4)
        sT = R1_t[:, 4 * T:8 * T].rearrange("p (c t) -> p c t", c=4)
        oT = R1_t[:, 8 * T:12 * T].rearrange("p (c t) -> p c t", c=4)
        mT = R1_t[:, 12 * T:20 * T].rearrange("p (c t) -> p c t", c=8)
        hid = R1_t[:].rearrange("p (c t) -> p c t", c=NHID)
        ident_bf = cbf_t[:, 0:128]
        ones_bf = cbf_t[:, 128:256]
        tri1_bf = cbf_t[:, 256:384]
        tri2_bf = cbf_t[:, 384:512]
        ident_f = cf_t[:, CF_IDENT:CF_IDENT + 128]
        mean_f = cf_t[:, CF_MEAN:CF_MEAN + 128]
        mask_f = cf_t[:, CF_MASK:CF_MASK + 128]
        eps_col = cf_t[:, CF_EPS:CF_EPS + 1]

        bank_rr = [0]

        def next_bank():
            i = bank_rr[0] % 7
            bank_rr[0] += 1
            return bank_t[i], Bbank[i]

        def stat_bank():
            return bank_t[7], Bbank[7]

        rr = {"evac": 0, "sq": 0, "tmp": 0, "sig": 0}

        def rot(name, n):
            i = rr[name] % n
            rr[name] += 1
            return i

        def dma(qname, out_ap, in_ap, reads, writes, sembuf):
            def fn(eng):
                return [eng.dma_start(out=out_ap, in_=in_ap)]
            fn.n = 1
            P.op(qname, fn, reads=reads, writes=writes, dma=sembuf)

        def dmas(qname, pairs, reads, writes, sembuf):
            def fn(eng):
                return [eng.dma_start(out=o, in_=i) for (o, i) in pairs]
            fn.n = len(pairs)
            P.op(qname, fn, reads=reads, writes=writes, dma=sembuf)

        def mm(out_ap, pairs, reads, wbuf, start=True, stop=True, skip=False):
            n = len(pairs)

            def fn(pe):
                last = None
                for i, (l, r) in enumerate(pairs):
                    last = pe.matmul(out_ap, l, r, start=(start and i == 0),
                                     stop=(stop and i == n - 1), skip_group_check=skip)
                return [last]
            P.op("pe", fn, reads=reads, writes=[wbuf])

        def act(out_ap, in_ap, func, reads, writes, bias=None, scale=None):
            def fn(eng):
                kw = {}
                if bias is not None:
                    kw["bias"] = bias
                if scale is not None:
                    kw["scale"] = scale
                return [eng.activation(out=out_ap, in_=in_ap, func=func, **kw)]
            P.op("act", fn, reads=reads, writes=writes)

        def tt(qname, out_ap, a, b, op, reads, writes):
            def fn(eng):
                return [eng.tensor_tensor(out=out_ap, in0=a, in1=b, op=op)]
            P.op(qname, fn, reads=reads, writes=writes)

        def ts(qname, out_ap, a, s1, s2, op0, op1, reads, writes):
            def fn(eng):
                if op1 is None:
                    return [eng.tensor_scalar(out=out_ap, in0=a, scalar1=s1, scalar2=None, op0=op0)]
                return [eng.tensor_scalar(out=out_ap, in0=a, scalar1=s1, scalar2=s2, op0=op0, op1=op1)]
            P.op(qname, fn, reads=reads, writes=writes)

        def stt(out_ap, a, s, b, op0, op1, reads, writes):
            def fn(eng):
                return [eng.scalar_tensor_tensor(out=out_ap, in0=a, scalar=s, in1=b, op0=op0, op1=op1)]
            P.op("dve", fn, reads=reads, writes=writes)

        def cp(qname, out_ap, in_ap, reads, writes):
            if qname == "act":
                act(out_ap, in_ap, AF.Copy, reads, writes)
                return

            def fn(eng):
                return [eng.tensor_copy(out=out_ap, in_=in_ap)]
            P.op(qname, fn, reads=reads, writes=writes)

        def mset(qname, ap, val, writes):
            def fn(eng):
                return [eng.memset(ap, val)]
            P.op(qname, fn, reads=[], writes=writes)

        dma("sp", cf_t[:], consts_d, [B["consts"]], [B["cf"]], B["cf"])
        dma("sp", cT_t[:], cT_d, [B["cT_d"]], [B["cT"]], B["cT"])
        cp("dve", cbf_t[:, 0:128], cf_t[:, CF_IDENT:CF_IDENT + 128], [B["cf"]], [B["cbf"]])
        cp("dve", cbf_t[:, 128:256], cf_t[:, CF_ONES:CF_ONES + 128], [B["cf"]], [B["cbf"]])
        cp("dve", cbf_t[:, 256:512], cf_t[:, CF_TRI1:CF_TRI1 + 256], [B["cf"]], [B["cbf"]])
        act(cact_t[:], cT_t[:], AF.Sigmoid, [B["cT"]], [B["cact"]])
        tt("dve", cact_t[:], cact_t[:], cT_t[:], ALU.mult, [B["cact"], B["cT"]], [B["cact"]])

        def wsrc_pieces(l, kind, idx):
            win = w_in_d[l].rearrange("(kc p) n -> p kc n", p=128)
            if kind == "glu":
                return [(win[:, :, idx * 128:(idx + 1) * 128], 0, 8, 128, 256),
                        (win[:, :, 512 + idx * 128:512 + (idx + 1) * 128], 128, 8, 128, 256)]
            if kind in ("q", "k", "v"):
                o = {"q": 1024, "k": 1536, "v": 2048}[kind]
                return [(win[:, :, o:o + 512], 0, 8, 512, 512)]
            if kind == "mix":
                wc = w_conv_out_d[l].rearrange("(kc p) n -> p kc n", p=128)
                wa = w_att_out_d[l].rearrange("(kc p) n -> p kc n", p=128)
                c0 = idx * 128
                return [(wc[:, :, c0:c0 + 128], 0, 4, 128, 128),
                        (wa[:, :, c0:c0 + 128], 4 * 128, 4, 128, 128),
                        (win[:, :, 2560 + c0:2560 + c0 + 128], 8 * 128, 8, 128, 128),
                        (win[:, :, 3584 + c0:3584 + c0 + 128], 16 * 128, 8, 128, 128)]
            if kind == "wo":
                wo = w_o_d[l].rearrange("(kc p) n -> p kc n", p=128)
                return [(wo[:, :, idx * 512:(idx + 1) * 512], 0, 8, 512, 512)]
            if kind == "fin":
                wf = w_ffn_in_d[l].rearrange("(kc p) n -> p kc n", p=128)
                return [(wf[:, :, idx * 256:(idx + 1) * 256], 0, 8, 256, 512),
                        (wf[:, :, DFF + idx * 256:DFF + (idx + 1) * 256], 256, 8, 256, 512)]
            if kind == "fout":
                wf = w_ffn_out_d[l].rearrange("(kc p) n -> p kc n", p=128)
                return [(wf[:, :, idx * 128:(idx + 1) * 128], 0, NHID, 128, 128)]
            raise ValueError(kind)

        def tile_elems(kind):
            return {"glu": 2048, "q": 4096, "k": 4096, "v": 4096, "mix": 3072, "wo": 4096,
                    "fin": 4096, "fout": NHID * 128}[kind]

        stage_t = [xT_t, y_t]
        stage_B = [B["xT"], B["y"]]
        cast_q = ["dve", "pool", "act"]
        n_pre = 0
        for l in range(DEPTH if STOP >= 1 else 0):
            for ti, (kind, idx) in enumerate(TILES):
                st = stage_t[n_pre % 2]
                sB = stage_B[n_pre % 2]
                slot = n_pre % NSLOT
                ne = tile_elems(kind)
                pairs = []
                for (src, off, a, n, rs) in wsrc_pieces(l, kind, idx):
                    dst = st[:, 0:a * rs].rearrange("p (a r) -> p a r", a=a)[:, :, off % rs:off % rs + n] \
                        if off < rs else st[:, off:off + a * rs].rearrange("p (a r) -> p a r", a=a)[:, :, 0:n]
                    pairs.append((dst, src))
                dmas("sp", pairs, [B["wsrc"]], [sB], sB)
                cp(cast_q[n_pre % 3], ring_t[slot][:, 0:ne], st[:, 0:ne], [sB], [Bring[slot]])
                dma("pool", wbf_d[l, ti, :, 0:ne], ring_t[slot][:, 0:ne], [Bring[slot]],
                    [Bwbf[l][ti]], Bring[slot])
                n_pre += 1

        stream = []
        for l in range(DEPTH):
            for s in range(NSEQ):
                for g in range(NG):
                    for ti in range(NT):
                        stream.append((l, ti))
        st_state = {"issued": 0, "consumed": 0}

        def issue_next():
            i = st_state["issued"]
            if i >= len(stream):
                return
            l, ti = stream[i]
            slot = i % NSLOT
            ne = tile_elems(TILES[ti][0])
            dma("sp", ring_t[slot][:, 0:ne], wbf_d[l, ti, :, 0:ne], [Bwbf[l][ti]], [Bring[slot]],
                Bring[slot])
            st_state["issued"] += 1

        def get_tile(l, ti):
            i = st_state["consumed"]
            assert stream[i] == (l, ti), (stream[i], l, ti)
            while st_state["issued"] <= i:
                issue_next()
            slot = i % NSLOT
            return ring_t[slot], Bring[slot]

        def done_tile():
            st_state["consumed"] += 1
            while st_state["issued"] < st_state["consumed"] + NSLOT - 1 + 1 and \
                    st_state["issued"] < len(stream):
                issue_next()

        def stats_sumsq(src_chunks, src_bufs, nchunks):
            bk, Bbk = stat_bank()
            pend = None
            for c in range(nchunks):
                i = rot("sq", 2)
                act(sq_t[i][:], src_chunks(c), AF.Square, src_bufs, [Bsq[i]])
                if pend is not None:
                    pc, pi = pend
                    mm(bk[:], [(ones_bf, sq_t[pi][:])], [B["cbf"], Bsq[pi]], Bbk,
                       start=(pc == 0), stop=False)
                pend = (c, i)
            pc, pi = pend
            mm(bk[:], [(ones_bf, sq_t[pi][:])], [B["cbf"], Bsq[pi]], Bbk, start=(pc == 0), stop=True)
            return bk, Bbk

        def rstd_from(bk, Bbk, inv_n):
            i = rot("tmp", 3)
            act(tmp_t[i][:], bk[:], AF.Ln, [Bbk, B["cf"]], [Btmp[i]], bias=eps_col, scale=inv_n)
            act(rstd_t[:], tmp_t[i][:], AF.Exp, [Btmp[i]], [B["rstd"]], scale=-0.5)

        def prenorm(b, which):
            bk, Bbk = stats_sumsq(lambda c: xT[:, c, :], [B["xT"]], NCH)
            rstd_from(bk, Bbk, 1.0 / D)
            shbase = 0 if which == 0 else 24
            for c in range(NCH):
                i = rot("tmp", 3)
                stt(tmp_t[i][:], xT[:, c, :], dsc_t[:, 2 * which, b, c:c + 1], rstd_t[:],
                    ALU.mult, ALU.mult, [B["xT"], B["dsc"], B["rstd"]], [Btmp[i]])
                ts("pool", hT_t[:, c, :], tmp_t[i][:], mod_t[:, shbase + c, b:b + 1], None,
                   ALU.add, None, [Btmp[i], B["mod"]], [B["hT"]])

        def postnorm_residual(b, which, produce):
            bk, Bbk = stat_bank()
            pend = None
            for c in range(NCH):
                ybk, Bybk = produce(c)
                i = rot("sq", 2)
                cp("dve", y3[:, c, :], ybk[:], [Bybk], [B["y"]])
                act(sq_t[i][:], y3[:, c, :], AF.Square, [B["y"]], [Bsq[i]])
                if pend is not None:
                    pc, pi = pend
                    mm(bk[:], [(ones_bf, sq_t[pi][:])], [B["cbf"], Bsq[pi]], Bbk,
                       start=(pc == 0), stop=False)
                pend = (c, i)
            pc, pi = pend
            mm(bk[:], [(ones_bf, sq_t[pi][:])], [B["cbf"], Bsq[pi]], Bbk, start=(pc == 0), stop=True)
            rstd_from(bk, Bbk, 1.0 / D)
            for c in range(NCH):
                i = rot("tmp", 3)
                stt(tmp_t[i][:], y3[:, c, :], dsc_t[:, 2 * which + 1, b, c:c + 1], rstd_t[:],
                    ALU.mult, ALU.mult, [B["y"], B["dsc"], B["rstd"]], [Btmp[i]])
                tt("pool", xT[:, c, :], xT[:, c, :], tmp_t[i][:], ALU.add, [B["xT"], Btmp[i]], [B["xT"]])

        def layer_setup(l):
            if STOP < 2:
                return
            dma("sp", spm_t[:], smallp_d[l], [B["smallp"]], [B["spm"]], B["spm"])
            bk, Bbk = next_bank()
            aw = ada_w_d[l].rearrange("(kc p) n -> p kc n", p=128)
            for piece in range(12):
                st = stage_t[piece % 2]
                sB = stage_B[piece % 2]
                stv = st[:].rearrange("p (kc n) -> p kc n", kc=NCH)
                dma("sp", stv, aw[:, :, piece * 512:(piece + 1) * 512], [B["ada_w"]], [sB], sB)
                for j in range(4):
                    ch = piece * 4 + j
                    mm(bk[:, ch * NSEQ:(ch + 1) * NSEQ],
                       [(stv[:, kc, j * 128:(j + 1) * 128], cact_t[:, kc, :]) for kc in range(NCH)],
                       [sB, B["cact"]], Bbk)
            modps = bk[:, 0:48 * NSEQ].rearrange("p (c b) -> p c b", b=NSEQ)
            for b in range(NSEQ):
                tt("dve", mod_t[:, :, b], modps[:, :, b], spm_t[:, SP_ADAB:SP_ADAB + 48], ALU.add,
                   [Bbk, B["spm"]], [B["mod"]])
                for which in range(2):
                    base = 24 * which
                    pre = SP_PRE1 if which == 0 else SP_PRE2
                    post = SP_POST1 if which == 0 else SP_POST2
                    stt(dsc_t[:, 2 * which, b, :], mod_t[:, base + 8:base + 16, b], 1.0,
                        spm_t[:, pre:pre + 8], ALU.add, ALU.mult, [B["mod"], B["spm"]], [B["dsc"]])
                    tt("dve", dsc_t[:, 2 * which + 1, b, :], mod_t[:, base + 16:base + 24, b],
                       spm_t[:, post:post + 8], ALU.mult, [B["mod"], B["spm"]], [B["dsc"]])

        def group(l, s, g):
            b = s
            tok0 = g * T
            last_layer = (l == DEPTH - 1)
            gbase = st_state["consumed"]

            def drain_tiles():
                while st_state["consumed"] < gbase + NT:
                    i = st_state["consumed"]
                    get_tile(*stream[i])
                    done_tile()
            if STOP < 3:
                return drain_tiles()
            if l == 0:
                dma("pool", ystage, x_d[s, tok0:tok0 + T, :].rearrange("(tb p) d -> p tb d", p=128),
                    [B["x_in"]], [B["y"]], B["y"])
                for c in range(NCH):
                    bk, Bbk = next_bank()

                    def fn(pe, bk=bk, c=c):
                        last = None
                        for tb in range(4):
                            last = pe.transpose(bk[:, tb * 128:(tb + 1) * 128],
                                                ystage[:, tb, c * 128:(c + 1) * 128], ident_f)
                        return [last]
                    P.op("pe", fn, reads=[B["y"], B["cf"]], writes=[Bbk])
                    cp("dve" if c % 2 == 0 else "act", xT[:, c, :], bk[:], [Bbk], [B["xT"]])
            else:
                dma("pool", xT_t[:], x1T_d[s, g], [Bx1[s][g]], [B["xT"]], B["xT"])

            prenorm(b, 0)

            if STOP < 4:
                return drain_tiles()
            if g == 0:
                mset("dve", uT_t[:, :, 0:HALO], 0.0, [B["uT"]])
            else:
                cp("dve", uT_t[:, :, 0:HALO], uT_t[:, :, T:T + HALO], [B["uT"]], [B["uT"]])
            for c in range(4):
                wt, Bwt = get_tile(l, c)
                bv, Bbv = next_bank()
                bg, Bbg = next_bank()
                mm(bv[:], [(wt[:, kc * 256:kc * 256 + 128], hT_t[:, kc, :]) for kc in range(NCH)],
                   [Bwt, B["hT"]], Bbv)
                mm(bg[:], [(wt[:, kc * 256 + 128:kc * 256 + 256], hT_t[:, kc, :]) for kc in range(NCH)],
                   [Bwt, B["hT"]], Bbg)
                done_tile()
                i = rot("sig", 2)
                act(sig_t[i][:], bg[:], AF.Sigmoid, [Bbg], [Bsig[i]])
                tt("dve", uT_t[:, c, HALO:HALO + T], bv[:], sig_t[i][:], ALU.mult, [Bbv, Bsig[i]], [B["uT"]])

            for c in range(4):
                for j in range(KCONV):
                    wcol = spm_t[:, SP_CONVW + j * 4 + c:SP_CONVW + j * 4 + c + 1]
                    if j == 0:
                        ts("pool", vconv[:, c, :], uT_t[:, c, 0:T], wcol,
                           spm_t[:, SP_CONVB + c:SP_CONVB + c + 1], ALU.mult, ALU.add,
                           [B["uT"], B["spm"]], [B["y"]])
                    else:
                        ts("pool", ptmp_t[:], uT_t[:, c, j:j + T], wcol, None, ALU.mult, None,
                           [B["uT"], B["spm"]], [B["ptmp"]])
                        tt("pool", vconv[:, c, :], vconv[:, c, :], ptmp_t[:], ALU.add,
                           [B["y"], B["ptmp"]], [B["y"]])

            if STOP < 5:
                return drain_tiles()
            wt, Bwt = get_tile(l, 4)
            for ch in range(4):
                bk, Bbk = next_bank()
                mm(bk[:], [(wt[:, kc * 512 + ch * 128:kc * 512 + (ch + 1) * 128], hT_t[:, kc, :])
                           for kc in range(NCH)], [Bwt, B["hT"]], Bbk)
                act(qT[:, ch, :], bk[:], AF.Copy, [Bbk], [B["R1"]], scale=0.125)
            done_tile()
            wt, Bwt = get_tile(l, 5)
            for ch in range(4):
                bk, Bbk = next_bank()
                mm(bk[:], [(wt[:, kc * 512 + ch * 128:kc * 512 + (ch + 1) * 128], hT_t[:, kc, :])
                           for kc in range(NCH)], [Bwt, B["hT"]], Bbk)
                cp("dve", kT_t[:, ch, tok0:tok0 + T], bk[:], [Bbk], [B["kT"]])
            done_tile()
            wt, Bwt = get_tile(l, 6)
            for tb in range(4):
                bk, Bbk = next_bank()
                mm(bk[:], [(hT_t[:, kc, tb * 128:(tb + 1) * 128], wt[:, kc * 512:(kc + 1) * 512])
                           for kc in range(NCH)], [Bwt, B["hT"]], Bbk)
                cp("dve" if tb % 2 == 0 else "act", V_t[:, 4 * g + tb, :], bk[:], [Bbk], [B["V"]])
            done_tile()

            zb = [(bank_t[0], Bbank[0]), (bank_t[1], Bbank[1])]
            accb = [(bank_t[2], Bbank[2]), (bank_t[3], Bbank[3])]
            ob = [(bank_t[4], Bbank[4]), (bank_t[5], Bbank[5])]
            items = []
            for ch in range(4):
                kbs = list(range(4 * g + 3, -1, -1))
                for ki, kb in enumerate(kbs):
                    for hh in range(2):
                        items.append((ch, hh, kb, ki == 0, ki == len(kbs) - 1))
            n_it = len(items)

            def geom(it):
                ch, hh, kb, first, lastk = it
                r = kb - 4 * g
                q0 = 128 * r if r >= 0 else 0
                return ch, hh, kb, first, lastk, r, q0, hh * 64

            def st0(i):
                ch, hh, kb, first, lastk, r, q0, pb = geom(items[i])
                z, Bz = zb[i % 2]
                mm(z[:, q0:T], [(kT_t[pb:pb + 64, ch, kb * 128:(kb + 1) * 128], qT[pb:pb + 64, ch, q0:T])],
                   [B["kT"], B["R1"]], Bz)

            def st1(i):
                ch, hh, kb, first, lastk, r, q0, pb = geom(items[i])
                z, Bz = zb[i % 2]
                e, Bei = e_t[i % 4], Be[i % 4]
                act(e[:, q0:T], z[:, q0:T], AF.Exp, [Bz], [Bei])
                if r >= 0:
                    tt("dve", e[:, q0:q0 + 128], e[:, q0:q0 + 128], mask_f, ALU.mult, [Bei, B["cf"]], [Bei])
                act(sp_t[i % 3][:, q0:T], e[:, q0:T], AF.Ln, [Bei], [Bsp[i % 3]], bias=1.0)

            def st2(i):
                ch, hh, kb, first, lastk, r, q0, pb = geom(items[i])
                acc, Bacc = accb[hh]
                mm(acc[:, q0:T], [(tri1_bf, sp_t[i % 3][:, q0:T])], [B["cbf"], Bsp[i % 3]], Bacc,
                   start=first, stop=False, skip=True)

            def st3(i):
                ch, hh, kb, first, lastk, r, q0, pb = geom(items[i])
                acc, Bacc = accb[hh]
                e, Bei = e_t[i % 4], Be[i % 4]
                ea, Bei2 = ea_t[i % 2], Bea[i % 2]
                act(ea[:, q0:T], acc[:, q0:T], AF.Exp, [Bacc], [Bei2])
                if not lastk:
                    mm(acc[:, q0:T], [(tri2_bf, sp_t[i % 3][:, q0:T])], [B["cbf"], Bsp[i % 3]], Bacc,
                       start=False, stop=False, skip=True)
                tt("dve", A_t[i % 2][:, q0:T], e[:, q0:T], ea[:, q0:T], ALU.mult, [Bei, Bei2], [BA[i % 2]])
                o, Bo = ob[ch % 2]
                mm(o[pb:pb + 64, q0:T], [(V_t[:, kb, (2 * ch + hh) * 64:(2 * ch + hh + 1) * 64],
                                          A_t[i % 2][:, q0:T])],
                   [B["V"], BA[i % 2]], Bo, start=first, stop=lastk, skip=True)
                if lastk and hh == 1:
                    cp("dve", oT[:, ch, :], o[:], [Bo], [B["R1"]])

            for step in range(n_it + 3):
                if 0 <= step - 3 < n_it:
                    st3(step - 3)
                if 0 <= step - 2 < n_it:
                    st2(step - 2)
                if 0 <= step - 1 < n_it:
                    st1(step - 1)
                if step < n_it:
                    st0(step)

            if STOP < 6:
                return drain_tiles()
            bk, Bbk = next_bank()
            mm(bk[:], [(mean_f, vconv[:, c, :]) for c in range(4)], [B["cf"], B["y"]], Bbk)
            if STOP < 6.05:
                return drain_tiles()
            for c in range(4):
                tt("dve", vconv[:, 4 + c, :], vconv[:, c, :], bk[:], ALU.subtract, [B["y"], Bbk], [B["y"]])
            for c in range(4):
                act(vconv[:, c, :], vconv[:, 4 + c, :], AF.Square, [B["y"]], [B["y"]])
            if STOP < 6.15:
                return drain_tiles()
            bk2, Bbk2 = next_bank()
            mm(bk2[:], [(mean_f, vconv[:, c, :]) for c in range(4)], [B["cf"], B["y"]], Bbk2)
            rstd_from(bk2, Bbk2, 1.0)
            if STOP < 6.25:
                return drain_tiles()
            for c in range(4):
                i = rot("tmp", 3)
                tt("dve", tmp_t[i][:], vconv[:, 4 + c, :], rstd_t[:], ALU.mult, [B["y"], B["rstd"]], [Btmp[i]])
                gcol = spm_t[:, SP_LNG + c:SP_LNG + c + 1]
                bcol = spm_t[:, SP_LNB + c:SP_LNB + c + 1]
                j = rot("sig", 2)
                act(sig_t[j][:], tmp_t[i][:], AF.Sigmoid, [Btmp[i], B["spm"]], [Bsig[j]], bias=bcol, scale=gcol)
                if STOP < 6.28:
                    continue
                ts("dve", tmp_t[i][:], tmp_t[i][:], gcol, bcol, ALU.mult, ALU.add,
                   [Btmp[i], B["spm"]], [Btmp[i]])
                tt("dve", sT[:, c, :], tmp_t[i][:], sig_t[j][:], ALU.mult, [Btmp[i], Bsig[j]], [B["R1"]])
            if STOP < 6.35:
                return drain_tiles()

            for c in range(8):
                wt, Bwt = get_tile(l, 7 + c)
                byc, Bbyc = next_bank()
                bya, Bbya = next_bank()
                bgc, Bbgc = next_bank()
                bga, Bbga = next_bank()
                mm(byc[:], [(wt[:, kc * 128:(kc + 1) * 128], sT[:, kc, :]) for kc in range(4)],
                   [Bwt, B["R1"]], Bbyc)
                mm(bya[:], [(wt[:, (4 + kc) * 128:(5 + kc) * 128], oT[:, kc, :]) for kc in range(4)],
                   [Bwt, B["R1"]], Bbya)
                mm(bgc[:], [(wt[:, (8 + kc) * 128:(9 + kc) * 128], hT_t[:, kc, :]) for kc in range(NCH)],
                   [Bwt, B["hT"]], Bbgc)
                mm(bga[:], [(wt[:, (16 + kc) * 128:(17 + kc) * 128], hT_t[:, kc, :]) for kc in range(NCH)],
                   [Bwt, B["hT"]], Bbga)
                done_tile()
                act(sig_t[0][:], bgc[:], AF.Sigmoid, [Bbgc], [Bsig[0]])
                act(sig_t[1][:], bga[:], AF.Sigmoid, [Bbga], [Bsig[1]])
                i = rot("tmp", 3)
                i2 = rot("tmp", 3)
                tt("dve", tmp_t[i][:], byc[:], sig_t[0][:], ALU.mult, [Bbyc, Bsig[0]], [Btmp[i]])
                tt("dve", tmp_t[i2][:], bya[:], sig_t[1][:], ALU.mult, [Bbya, Bsig[1]], [Btmp[i2]])
                tt("pool", mT[:, c, :], tmp_t[i][:], tmp_t[i2][:], ALU.add, [Btmp[i], Btmp[i2]], [B["R1"]])

            if STOP < 6.45:
                return drain_tiles()
            wo_state = {}

            def produce_wo(c):
                if c % 4 == 0:
                    if c > 0:
                        done_tile()
                    wo_state["t"] = get_tile(l, 15 + c // 4)
                wt, Bwt = wo_state["t"]
                bk, Bbk = next_bank()
                cc = c % 4
                mm(bk[:], [(wt[:, kc * 512 + cc * 128:kc * 512 + (cc + 1) * 128], mT[:, kc, :])
                           for kc in range(NCH)], [Bwt, B["R1"]], Bbk)
                return bk, Bbk
            postnorm_residual(b, 0, produce_wo)
            done_tile()

            if STOP < 7:
                return drain_tiles()
            prenorm(b, 1)
            for i in range(11):
                wt, Bwt = get_tile(l, 17 + i)
                bks = [next_bank() for _ in range(4)]
                for w in range(2):
                    for jj in range(2):
                        bk, Bbk = bks[w * 2 + jj]
                        mm(bk[:], [(wt[:, kc * 512 + w * 256 + jj * 128:kc * 512 + w * 256 + (jj + 1) * 128],
                                    hT_t[:, kc, :]) for kc in range(NCH)], [Bwt, B["hT"]], Bbk)
                done_tile()
                for jj in range(2):
                    gk, Bgk = bks[jj]
                    uk, Buk = bks[2 + jj]
                    si = rot("sig", 2)
                    ti_ = rot("tmp", 3)
                    act(sig_t[si][:], gk[:], AF.Sigmoid, [Bgk], [Bsig[si]])
                    tt("dve", tmp_t[ti_][:], gk[:], sig_t[si][:], ALU.mult, [Bgk, Bsig[si]], [Btmp[ti_]])
                    tt("dve", hid[:, 2 * i + jj, :], uk[:], tmp_t[ti_][:], ALU.mult, [Buk, Btmp[ti_]], [B["R1"]])

            def produce_fout(c):
                wt, Bwt = get_tile(l, 28 + c)
                bk, Bbk = next_bank()
                mm(bk[:], [(wt[:, kc * 128:(kc + 1) * 128], hid[:, kc, :]) for kc in range(NHID)],
                   [Bwt, B["R1"]], Bbk)
                done_tile()
                return bk, Bbk
            postnorm_residual(b, 1, produce_fout)

            if not last_layer:
                dma("pool", x1T_d[s, g], xT_t[:], [B["xT"]], [Bx1[s][g]], B["xT"])
            else:
                for tb in range(4):
                    for half in range(2):
                        bk, Bbk = next_bank()

                        def fn(pe, bk=bk, tb=tb, half=half):
                            last = None
                            for cc in range(4):
                                last = pe.transpose(bk[:, cc * 128:(cc + 1) * 128],
                                                    xT[:, half * 4 + cc, tb * 128:(tb + 1) * 128], ident_f)
                            return [last]
                        P.op("pe", fn, reads=[B["xT"], B["cf"]], writes=[Bbk])
                        cp("dve" if half == 0 else "act", ystage[:, tb, half * 512:(half + 1) * 512], bk[:],
                           [Bbk], [B["y"]])
                dma("pool", out_d[s, tok0:tok0 + T, :].rearrange("(tb p) d -> p tb d", p=128), ystage,
                    [B["y"]], [B["out"]], B["y"])

        for l in range(DEPTH):
            layer_setup(l)
            for s in range(NSEQ):
                for g in range(NG):
                    group(l, s, g)
        if STOP < 99:
            dma("pool", out_d[0, 0:128, :], y_t[:, 0:D], [B["y"]], [B["out"]], B["y"])
        P.final_wait("pool", [B["y"], B["out"]])

        with nc.Block() as block:
            @block.tensor
            def _(eng):
                P.emit("pe", eng)

            @block.scalar
            def _(eng):
                P.emit("act", eng)

            @block.vector
            def _(eng):
                P.emit("dve", eng)

            @block.gpsimd
            def _(eng):
                P.emit("pool", eng)

            @block.sync
            def _(eng):
                P.emit("sp", eng)
    return nc


def make_consts():
    cf = np.zeros((128, NCF), np.float32)
    idx = np.arange(128)
    cf[:, CF_IDENT:CF_IDENT + 128] = np.eye(128, dtype=np.float32)
    cf[:, CF_ONES:CF_ONES + 128] = 1.0
    cf[:, CF_MEAN:CF_MEAN + 128] = 1.0 / CW
    cf[:, CF_MASK:CF_MASK + 128] = (idx[:, None] < idx[None, :]).astype(np.float32)
    cf[:, CF_TRI1:CF_TRI1 + 128] = -(idx[:, None] >= idx[None, :]).astype(np.float32)
    cf[:, CF_TRI2:CF_TRI2 + 128] = -(idx[:, None] < idx[None, :]).astype(np.float32)
    cf[:, CF_EPS:CF_EPS + 4] = EPS
    return cf


def make_smallp(depth, pre_mix_g, post_mix_g, pre_ffn_g, post_ffn_g, ada_b, conv_b, conv_ln_g,
                conv_ln_b, conv_w):
    sp = np.zeros((depth, 128, NSP), np.float32)

    def fm(v):
        return np.ascontiguousarray(v.reshape(-1, 128).T)
    for l in range(depth):
        sp[l, :, SP_PRE1:SP_PRE1 + 8] = fm(pre_mix_g[l])
        sp[l, :, SP_POST1:SP_POST1 + 8] = fm(post_mix_g[l])
        sp[l, :, SP_PRE2:SP_PRE2 + 8] = fm(pre_ffn_g[l])
        sp[l, :, SP_POST2:SP_POST2 + 8] = fm(post_ffn_g[l])
        sp[l, :, SP_ADAB:SP_ADAB + 48] = fm(ada_b[l])
        sp[l, :, SP_CONVB:SP_CONVB + 4] = fm(conv_b[l])
        sp[l, :, SP_LNG:SP_LNG + 4] = fm(conv_ln_g[l])
        sp[l, :, SP_LNB:SP_LNB + 4] = fm(conv_ln_b[l])
        cw = conv_w[l].reshape(KCONV, 4, 128).transpose(2, 0, 1)
        sp[l, :, SP_CONVW:SP_CONVW + KCONV * 4] = cw.reshape(128, KCONV * 4)
    return sp


def run(inputs, nseq, S, depth, ncores, trace=False, stop=99):
    x = np.asarray(inputs["x"], np.float32)
    c = np.asarray(inputs["c"], np.float32)
    f = lambda k: np.ascontiguousarray(np.asarray(inputs[k], np.float32))
    smallp = make_smallp(depth, f("pre_mix_g"), f("post_mix_g"), f("pre_ffn_g"), f("post_ffn_g"),
                         f("ada_b"), f("conv_b"), f("conv_ln_g"), f("conv_ln_b"), f("conv_w"))
    consts = make_consts()
    shared = {"ada_w": f("ada_w")[:depth], "smallp": smallp, "consts": consts, "w_in": f("w_in")[:depth],
              "w_conv_out": f("w_conv_out")[:depth], "w_att_out": f("w_att_out")[:depth],
              "w_o": f("w_o")[:depth], "w_ffn_in": f("w_ffn_in")[:depth],
              "w_ffn_out": f("w_ffn_out")[:depth]}
    in_maps = []
    for k in range(ncores):
        xs = np.ascontiguousarray(x[k * nseq:(k + 1) * nseq])
        cs = c[k * nseq:(k + 1) * nseq]
        cT = np.ascontiguousarray(cs.reshape(nseq, NCH, 128).transpose(2, 1, 0))
        m = dict(shared)
        m["x"] = xs
        m["cT"] = cT
        in_maps.append(m)
    nc = build(nseq, S, depth, stop)
    res = run_bass_kernel_spmd(nc, in_maps, core_ids=list(range(ncores)), trace=trace)
    out = np.concatenate([np.asarray(r["out"]) for r in res.results], axis=0)
    return out.astype(np.float32), res


def kernel(**inputs):
    out, _ = run(inputs, 2, 4096, 2, NCORES)
    return out
```

```python
import contextlib
import numpy as np
import concourse.bass as bass
import concourse.mybir as mybir
from concourse.bass_utils import run_bass_kernel_spmd

F32 = mybir.dt.float32
BF16 = mybir.dt.bfloat16
AF = mybir.ActivationFunctionType
ALU = mybir.AluOpType

D = 1024
NCH = 8
T = 512
CW = 512
NH = 8
HD = 64
DFF = 2816
NHID = 22
KCONV = 31
HALO = KCONV - 1
INC = 4608
EPS = 1e-6
NSLOT = 4
SLOT = 4096
NCORES = 8

TILES = ([("glu", c) for c in range(4)] + [("q", 0), ("k", 0), ("v", 0)] +
         [("mix", c) for c in range(8)] + [("wo", t) for t in range(2)] +
         [("fin", i) for i in range(11)] + [("fout", c) for c in range(8)])
NT = len(TILES)

SP_PRE1, SP_POST1, SP_PRE2, SP_POST2 = 0, 8, 16, 24
SP_ADAB = 32
SP_CONVB = 80
SP_LNG = 84
SP_LNB = 88
SP_CONVW = 92
NSP = 92 + KCONV * 4

CF_IDENT = 0
CF_ONES = 128
CF_MEAN = 256
CF_MASK = 384
CF_TRI1 = 512
CF_TRI2 = 640
CF_EPS = 768
NCF = 772


class Buf:
    __slots__ = ("name", "w", "r", "semkey", "dcnt")

    def __init__(self, name):
        self.name = name
        self.w = None
        self.r = []
        self.semkey = None
        self.dcnt = 0


class Queue:
    def __init__(self, name, semkey, is_pe=False):
        self.name = name
        self.semkey = semkey
        self.cnt = 0
        self.ops = []
        self.seen = {}
        self.is_pe = is_pe


class Prog:
    def __init__(self, nc, stack):
        self.nc = nc
        self.stack = stack
        self.sems = []
        self.dma_sem_cnt = {}
        self.queues = {}
        for name in ("pe", "act", "dve", "pool", "sp"):
            k = self.new_sem(name)
            self.queues[name] = Queue(name, k, is_pe=(name == "pe"))

    def new_sem(self, name):
        s = self.stack.enter_context(self.nc.semaphore("s_" + name))
        self.sems.append(s)
        return len(self.sems) - 1

    def op(self, qname, fn, reads=(), writes=(), dma=None):
        q = self.queues[qname]
        need = {}

        def add(ev):
            if ev is None:
                return
            k, v = ev
            if need.get(k, 0) < v:
                need[k] = v

        for b in reads:
            add(b.w)
        for b in writes:
            add(b.w)
            for ev in b.r:
                add(ev)
        waits = []
        for k, v in need.items():
            if q.is_pe and k == q.semkey:
                continue
            if k in self.dma_sem_cnt:
                v = max(v, self.dma_sem_cnt[k])
            if q.seen.get(k, 0) >= v:
                continue
            q.seen[k] = v
            waits.append((k, v))
        if dma is not None:
            if dma.semkey is None:
                dma.semkey = {}
            if qname not in dma.semkey:
                dma.semkey[qname] = self.new_sem("d_" + dma.name + "_" + qname)
                self.dma_sem_cnt[dma.semkey[qname]] = 0
            semkey = dma.semkey[qname]
        else:
            semkey = q.semkey
        rec = {"waits": waits, "fn": fn, "semkey": semkey, "dma": dma is not None}
        q.ops.append(rec)
        if dma is not None:
            n = getattr(fn, "n", 1)
            self.dma_sem_cnt[semkey] += 16 * n
            ev = (semkey, self.dma_sem_cnt[semkey])
        else:
            q.cnt += 1
            ev = (semkey, q.cnt)
        for b in writes:
            b.w = ev
            b.r = []
        for b in reads:
            b.r.append(ev)
        return ev

    def final_wait(self, qname, bufs):
        q = self.queues[qname]
        need = {}
        for b in bufs:
            for ev in [b.w] + list(b.r):
                if ev is None:
                    continue
                k, v = ev
                if k in self.dma_sem_cnt:
                    v = max(v, self.dma_sem_cnt[k])
                need[k] = max(need.get(k, 0), v)
        q.ops.append({"waits": list(need.items()), "fn": None, "semkey": None, "dma": False})

    def emit(self, qname, eng):
        q = self.queues[qname]
        for rec in q.ops:
            for k, v in rec["waits"]:
                eng.wait_ge(self.sems[k], v)
            if rec["fn"] is None:
                continue
            insts = rec["fn"](eng)
            sem = self.sems[rec["semkey"]]
            if rec["dma"]:
                for ins in insts:
                    ins.then_inc(sem, 16)
            else:
                insts[-1].then_inc(sem, 1)


def build(NSEQ, S, DEPTH, STOP=99):
    NG = S // T
    NKB = S // 128
    nc = bass.Bass("TRN2", target_bir_lowering=False)

    def din(name, shape, dt=F32):
        return nc.dram_tensor(name, list(shape), dt, kind="ExternalInput").ap()

    x_d = din("x", [NSEQ, S, D])
    cT_d = din("cT", [128, NCH, NSEQ])
    ada_w_d = din("ada_w", [DEPTH, D, 6 * D])
    smallp_d = din("smallp", [DEPTH, 128, NSP])
    consts_d = din("consts", [128, NCF])
    w_in_d = din("w_in", [DEPTH, D, INC])
    w_conv_out_d = din("w_conv_out", [DEPTH, CW, D])
    w_att_out_d = din("w_att_out", [DEPTH, CW, D])
    w_o_d = din("w_o", [DEPTH, D, D])
    w_ffn_in_d = din("w_ffn_in", [DEPTH, D, 2 * DFF])
    w_ffn_out_d = din("w_ffn_out", [DEPTH, DFF, D])
    out_d = nc.dram_tensor("out", [NSEQ, S, D], F32, kind="ExternalOutput").ap()
    wbf_d = nc.dram_tensor("wbf", [DEPTH, NT, 128, SLOT], BF16, kind="Internal").ap()
    x1T_d = nc.dram_tensor("x1T", [NSEQ, NG, 128, NCH * T], F32, kind="Internal").ap()

    with contextlib.ExitStack() as stack:
        def sb(name, shape, dt):
            return stack.enter_context(nc.sbuf_tensor(name, list(shape), dt))

        def ps(name):
            return stack.enter_context(nc.psum_tensor(name, [128, 512], F32))

        kT_t = sb("kT", [128, 4, S], BF16)
        V_t = sb("Vc", [128, NKB, 512], BF16)
        hT_t = sb("hT", [128, NCH, T], BF16)
        ring_t = [sb("ring%d" % i, [128, SLOT], BF16) for i in range(NSLOT)]
        R1_t = sb("R1", [128, NHID * T], BF16)
        uT_t = sb("uT", [128, 4, HALO + T], BF16)
        sp_t = [sb("sp%d" % i, [128, T], BF16) for i in range(3)]
        A_t = [sb("A%d" % i, [128, T], BF16) for i in range(2)]
        sq_t = [sb("sq%d" % i, [128, T], BF16) for i in range(2)]
        cbf_t = sb("cbf", [128, 512], BF16)
        xT_t = sb("xT", [128, NCH * T], F32)
        y_t = sb("y", [128, NCH * T], F32)
        e_t = [sb("e%d" % i, [128, T], F32) for i in range(4)]
        ea_t = [sb("ea%d" % i, [128, T], F32) for i in range(2)]
        rstd_t = sb("rstd", [128, T], F32)
        tmp_t = [sb("tmp%d" % i, [128, T], F32) for i in range(3)]
        sig_t = [sb("sig%d" % i, [128, T], F32) for i in range(2)]
        ptmp_t = [sb("ptmp%d" % i, [128, T], F32) for i in range(2)]
        cf_t = sb("cf", [128, NCF], F32)
        spm_t = sb("spm", [128, NSP], F32)
        cT_t = sb("cTs", [128, NCH, NSEQ], F32)
        cact_t = sb("cact", [128, NCH, NSEQ], F32)
        mod_t = sb("mod", [128, 48, NSEQ], F32)
        dsc_t = sb("dsc", [128, 4, NSEQ, NCH], F32)
        bank_t = [ps("bank%d" % i) for i in range(8)]

        P = Prog(nc, stack)

        B = {}
        for nm in ["kT", "V", "hT", "R1", "uT", "cbf", "xT", "y", "rstd", "ptmp", "cf", "spm",
                   "cT", "cact", "mod", "dsc", "x_in", "out", "ada_w", "smallp", "consts",
                   "wsrc", "cT_d"]:
            B[nm] = Buf(nm)
        Bring = [Buf("ring%d" % i) for i in range(NSLOT)]
        Bsp = [Buf("sp%d" % i) for i in range(3)]
        BA = [Buf("A%d" % i) for i in range(2)]
        Bsq = [Buf("sq%d" % i) for i in range(2)]
        Be = [Buf("e%d" % i) for i in range(4)]
        Bea = [Buf("ea%d" % i) for i in range(2)]
        Btmp = [Buf("tmp%d" % i) for i in range(3)]
        Bsig = [Buf("sig%d" % i) for i in range(2)]
        Bbank = [Buf("bank%d" % i) for i in range(8)]
        Bptmp = [Buf("ptmp%d" % i) for i in range(2)]
        Bwbf = [[Buf("wbf%d_%d" % (l, t)) for t in range(NT)] for l in range(DEPTH)]
        Bx1 = [[Buf("x1_%d_%d" % (s, g)) for g in range(NG)] for s in range(NSEQ)]

        xT = xT_t[:].rearrange("p (c t) -> p c t", c=NCH)
        y3 = y_t[:].rearrange("p (c t) -> p c t", c=NCH)
        ystage = y_t[:].rearrange("p (b d) -> p b d", b=4)
        vconv = y3
        qT = R1_t[:, 0:4 * T].rearrange("p (c t) -> p c t", c=4)
        sT = R1_t[:, 4 * T:8 * T].rearrange("p (c t) -> p c t", c=4)
        oT = R1_t[:, 8 * T:12 * T].rearrange("p (c t) -> p c t", c=4)
        mT = R1_t[:, 12 * T:20 * T].rearrange("p (c t) -> p c t", c=8)
        hid = R1_t[:].rearrange("p (c t) -> p c t", c=NHID)
        ident_bf = cbf_t[:, 0:128]
        ones_bf = cbf_t[:, 128:256]
        tri1_bf = cbf_t[:, 256:384]
        tri2_bf = cbf_t[:, 384:512]
        ident_f = cf_t[:, CF_IDENT:CF_IDENT + 128]
        mean_f = cf_t[:, CF_MEAN:CF_MEAN + 128]
        mask_f = cf_t[:, CF_MASK:CF_MASK + 128]
        eps_col = cf_t[:, CF_EPS:CF_EPS + 1]

        bank_rr = [0]

        def next_bank():
            i = bank_rr[0] % 7
            bank_rr[0] += 1
            return bank_t[i], Bbank[i]

        def stat_bank():
            return bank_t[7], Bbank[7]

        rr = {"evac": 0, "sq": 0, "tmp": 0, "sig": 0, "ptmp": 0}

        def rot(name, n):
            i = rr[name] % n
            rr[name] += 1
            return i

        def dma(qname, out_ap, in_ap, reads, writes, sembuf):
            def fn(eng):
                return [eng.dma_start(out=out_ap, in_=in_ap)]
            fn.n = 1
            P.op(qname, fn, reads=reads, writes=writes, dma=sembuf)

        def dmas(qname, pairs, reads, writes, sembuf):
            def fn(eng):
                return [eng.dma_start(out=o, in_=i) for (o, i) in pairs]
            fn.n = len(pairs)
            P.op(qname, fn, reads=reads, writes=writes, dma=sembuf)

        def mm(out_ap, pairs, reads, wbuf, start=True, stop=True, skip=False):
            n = len(pairs)

            def fn(pe):
                last = None
                for i, (l, r) in enumerate(pairs):
                    last = pe.matmul(out_ap, l, r, start=(start and i == 0),
                                     stop=(stop and i == n - 1), skip_group_check=skip)
                return [last]
            P.op("pe", fn, reads=reads, writes=[wbuf])

        def act(out_ap, in_ap, func, reads, writes, bias=None, scale=None):
            def fn(eng):
                kw = {}
                if bias is not None:
                    kw["bias"] = bias
                if scale is not None:
                    kw["scale"] = scale
                return [eng.activation(out=out_ap, in_=in_ap, func=func, **kw)]
            P.op("act", fn, reads=reads, writes=writes)

        def tt(qname, out_ap, a, b, op, reads, writes):
            def fn(eng):
                return [eng.tensor_tensor(out=out_ap, in0=a, in1=b, op=op)]
            P.op(qname, fn, reads=reads, writes=writes)

        def ts(qname, out_ap, a, s1, s2, op0, op1, reads, writes):
            def fn(eng):
                if op1 is None:
                    return [eng.tensor_scalar(out=out_ap, in0=a, scalar1=s1, scalar2=None, op0=op0)]
                return [eng.tensor_scalar(out=out_ap, in0=a, scalar1=s1, scalar2=s2, op0=op0, op1=op1)]
            P.op(qname, fn, reads=reads, writes=writes)

        def stt(out_ap, a, s, b, op0, op1, reads, writes):
            def fn(eng):
                return [eng.scalar_tensor_tensor(out=out_ap, in0=a, scalar=s, in1=b, op0=op0, op1=op1)]
            P.op("dve", fn, reads=reads, writes=writes)

        def cp(qname, out_ap, in_ap, reads, writes):
            if qname == "act":
                act(out_ap, in_ap, AF.Copy, reads, writes)
                return

            def fn(eng):
                return [eng.tensor_copy(out=out_ap, in_=in_ap)]
            P.op(qname, fn, reads=reads, writes=writes)

        def mset(qname, ap, val, writes):
            def fn(eng):
                return [eng.memset(ap, val)]
            P.op(qname, fn, reads=[], writes=writes)

        dma("sp", cf_t[:], consts_d, [B["consts"]], [B["cf"]], B["cf"])
        dma("sp", cT_t[:], cT_d, [B["cT_d"]], [B["cT"]], B["cT"])
        cp("dve", cbf_t[:, 0:128], cf_t[:, CF_IDENT:CF_IDENT + 128], [B["cf"]], [B["cbf"]])
        cp("dve", cbf_t[:, 128:256], cf_t[:, CF_ONES:CF_ONES + 128], [B["cf"]], [B["cbf"]])
        cp("dve", cbf_t[:, 256:512], cf_t[:, CF_TRI1:CF_TRI1 + 256], [B["cf"]], [B["cbf"]])
        act(cact_t[:], cT_t[:], AF.Sigmoid, [B["cT"]], [B["cact"]])
        tt("dve", cact_t[:], cact_t[:], cT_t[:], ALU.mult, [B["cact"], B["cT"]], [B["cact"]])

        def wsrc_pieces(l, kind, idx):
            win = w_in_d[l].rearrange("(kc p) n -> p kc n", p=128)
            if kind == "glu":
                return [(win[:, :, idx * 128:(idx + 1) * 128], 0, 8, 128, 256),
                        (win[:, :, 512 + idx * 128:512 + (idx + 1) * 128], 128, 8, 128, 256)]
            if kind in ("q", "k", "v"):
                o = {"q": 1024, "k": 1536, "v": 2048}[kind]
                return [(win[:, :, o:o + 512], 0, 8, 512, 512)]
            if kind == "mix":
                wc = w_conv_out_d[l].rearrange("(kc p) n -> p kc n", p=128)
                wa = w_att_out_d[l].rearrange("(kc p) n -> p kc n", p=128)
                c0 = idx * 128
                return [(wc[:, :, c0:c0 + 128], 0, 4, 128, 128),
                        (wa[:, :, c0:c0 + 128], 4 * 128, 4, 128, 128),
                        (win[:, :, 2560 + c0:2560 + c0 + 128], 8 * 128, 8, 128, 128),
                        (win[:, :, 3584 + c0:3584 + c0 + 128], 16 * 128, 8, 128, 128)]
            if kind == "wo":
                wo = w_o_d[l].rearrange("(kc p) n -> p kc n", p=128)
                return [(wo[:, :, idx * 512:(idx + 1) * 512], 0, 8, 512, 512)]
            if kind == "fin":
                wf = w_ffn_in_d[l].rearrange("(kc p) n -> p kc n", p=128)
                return [(wf[:, :, idx * 256:(idx + 1) * 256], 0, 8, 256, 512),
                        (wf[:, :, DFF + idx * 256:DFF + (idx + 1) * 256], 256, 8, 256, 512)]
            if kind == "fout":
                wf = w_ffn_out_d[l].rearrange("(kc p) n -> p kc n", p=128)
                return [(wf[:, :, idx * 128:(idx + 1) * 128], 0, NHID, 128, 128)]
            raise ValueError(kind)

        def tile_elems(kind):
            return {"glu": 2048, "q": 4096, "k": 4096, "v": 4096, "mix": 3072, "wo": 4096,
                    "fin": 4096, "fout": NHID * 128}[kind]

        stage_t = [xT_t, y_t]
        stage_B = [B["xT"], B["y"]]
        cast_q = ["dve", "pool", "act"]
        n_pre = 0
        for l in range(DEPTH if STOP >= 1 else 0):
            for ti, (kind, idx) in enumerate(TILES):
                st = stage_t[n_pre % 2]
                sB = stage_B[n_pre % 2]
                slot = n_pre % NSLOT
                ne = tile_elems(kind)
                pairs = []
                for (src, off, a, n, rs) in wsrc_pieces(l, kind, idx):
                    dst = st[:, 0:a * rs].rearrange("p (a r) -> p a r", a=a)[:, :, off % rs:off % rs + n] \
                        if off < rs else st[:, off:off + a * rs].rearrange("p (a r) -> p a r", a=a)[:, :, 0:n]
                    pairs.append((dst, src))
                dmas("sp", pairs, [B["wsrc"]], [sB], sB)
                cp(cast_q[n_pre % 3], ring_t[slot][:, 0:ne], st[:, 0:ne], [sB], [Bring[slot]])
                dma("pool", wbf_d[l, ti, :, 0:ne], ring_t[slot][:, 0:ne], [Bring[slot]],
                    [Bwbf[l][ti]], Bring[slot])
                n_pre += 1

        stream = []
        for l in range(DEPTH):
            for s in range(NSEQ):
                for g in range(NG):
                    for ti in range(NT):
                        stream.append((l, ti))
        st_state = {"issued": 0, "consumed": 0}

        def issue_next():
            i = st_state["issued"]
            if i >= len(stream):
                return
            l, ti = stream[i]
            slot = i % NSLOT
            ne = tile_elems(TILES[ti][0])
            dma("sp", ring_t[slot][:, 0:ne], wbf_d[l, ti, :, 0:ne], [Bwbf[l][ti]], [Bring[slot]],
                Bring[slot])
            st_state["issued"] += 1

        def get_tile(l, ti):
            i = st_state["consumed"]
            assert stream[i] == (l, ti), (stream[i], l, ti)
            while st_state["issued"] <= i:
                issue_next()
            slot = i % NSLOT
            return ring_t[slot], Bring[slot]

        def done_tile():
            st_state["consumed"] += 1
            while st_state["issued"] < st_state["consumed"] + NSLOT - 1 + 1 and \
                    st_state["issued"] < len(stream):
                issue_next()

        def stats_sumsq(src_chunks, src_bufs, nchunks):
            bk, Bbk = stat_bank()
            pend = None
            for c in range(nchunks):
                i = rot("sq", 2)
                act(sq_t[i][:], src_chunks(c), AF.Square, src_bufs, [Bsq[i]])
                if pend is not None:
                    pc, pi = pend
                    mm(bk[:], [(ones_bf, sq_t[pi][:])], [B["cbf"], Bsq[pi]], Bbk,
                       start=(pc == 0), stop=False)
                pend = (c, i)
            pc, pi = pend
            mm(bk[:], [(ones_bf, sq_t[pi][:])], [B["cbf"], Bsq[pi]], Bbk, start=(pc == 0), stop=True)
            return bk, Bbk

        def rstd_from(bk, Bbk, inv_n):
            i = rot("tmp", 3)
            act(tmp_t[i][:], bk[:], AF.Ln, [Bbk, B["cf"]], [Btmp[i]], bias=eps_col, scale=inv_n)
            act(rstd_t[:], tmp_t[i][:], AF.Exp, [Btmp[i]], [B["rstd"]], scale=-0.5)

        def prenorm(b, which):
            bk, Bbk = stats_sumsq(lambda c: xT[:, c, :], [B["xT"]], NCH)
            rstd_from(bk, Bbk, 1.0 / D)
            shbase = 0 if which == 0 else 24
            for c in range(NCH):
                i = rot("tmp", 3)
                stt(tmp_t[i][:], xT[:, c, :], dsc_t[:, 2 * which, b, c:c + 1], rstd_t[:],
                    ALU.mult, ALU.mult, [B["xT"], B["dsc"], B["rstd"]], [Btmp[i]])
                ts("dve", hT_t[:, c, :], tmp_t[i][:], mod_t[:, shbase + c, b:b + 1], None,
                   ALU.add, None, [Btmp[i], B["mod"]], [B["hT"]])

        def postnorm_residual(b, which, produce):
            bk, Bbk = stat_bank()
            pend = None
            for c in range(NCH):
                ybk, Bybk = produce(c)
                i = rot("sq", 2)
                cp("dve", y3[:, c, :], ybk[:], [Bybk], [B["y"]])
                act(sq_t[i][:], y3[:, c, :], AF.Square, [B["y"]], [Bsq[i]])
                if pend is not None:
                    pc, pi = pend
                    mm(bk[:], [(ones_bf, sq_t[pi][:])], [B["cbf"], Bsq[pi]], Bbk,
                       start=(pc == 0), stop=False)
                pend = (c, i)
            pc, pi = pend
            mm(bk[:], [(ones_bf, sq_t[pi][:])], [B["cbf"], Bsq[pi]], Bbk, start=(pc == 0), stop=True)
            rstd_from(bk, Bbk, 1.0 / D)
            for c in range(NCH):
                i = rot("tmp", 3)
                stt(tmp_t[i][:], y3[:, c, :], dsc_t[:, 2 * which + 1, b, c:c + 1], rstd_t[:],
                    ALU.mult, ALU.mult, [B["y"], B["dsc"], B["rstd"]], [Btmp[i]])
                tt("pool", xT[:, c, :], xT[:, c, :], tmp_t[i][:], ALU.add, [B["xT"], Btmp[i]], [B["xT"]])

        def layer_setup(l):
            if STOP < 2:
                return
            dma("sp", spm_t[:], smallp_d[l], [B["smallp"]], [B["spm"]], B["spm"])
            bk, Bbk = next_bank()
            aw = ada_w_d[l].rearrange("(kc p) n -> p kc n", p=128)
            for piece in range(12):
                st = stage_t[piece % 2]
                sB = stage_B[piece % 2]
                stv = st[:].rearrange("p (kc n) -> p kc n", kc=NCH)
                dma("sp", stv, aw[:, :, piece * 512:(piece + 1) * 512], [B["ada_w"]], [sB], sB)
                for j in range(4):
                    ch = piece * 4 + j
                    mm(bk[:, ch * NSEQ:(ch + 1) * NSEQ],
                       [(stv[:, kc, j * 128:(j + 1) * 128], cact_t[:, kc, :]) for kc in range(NCH)],
                       [sB, B["cact"]], Bbk)
            modps = bk[:, 0:48 * NSEQ].rearrange("p (c b) -> p c b", b=NSEQ)
            for b in range(NSEQ):
                tt("dve", mod_t[:, :, b], modps[:, :, b], spm_t[:, SP_ADAB:SP_ADAB + 48], ALU.add,
                   [Bbk, B["spm"]], [B["mod"]])
                for which in range(2):
                    base = 24 * which
                    pre = SP_PRE1 if which == 0 else SP_PRE2
                    post = SP_POST1 if which == 0 else SP_POST2
                    stt(dsc_t[:, 2 * which, b, :], mod_t[:, base + 8:base + 16, b], 1.0,
                        spm_t[:, pre:pre + 8], ALU.add, ALU.mult, [B["mod"], B["spm"]], [B["dsc"]])
                    tt("dve", dsc_t[:, 2 * which + 1, b, :], mod_t[:, base + 16:base + 24, b],
                       spm_t[:, post:post + 8], ALU.mult, [B["mod"], B["spm"]], [B["dsc"]])

        def group(l, s, g):
            b = s
            tok0 = g * T
            last_layer = (l == DEPTH - 1)
            gbase = st_state["consumed"]

            def drain_tiles():
                while st_state["consumed"] < gbase + NT:
                    i = st_state["consumed"]
                    get_tile(*stream[i])
                    done_tile()
            if STOP < 3:
                return drain_tiles()
            if l == 0:
                dma("pool", ystage, x_d[s, tok0:tok0 + T, :].rearrange("(tb p) d -> p tb d", p=128),
                    [B["x_in"]], [B["y"]], B["y"])
                for c in range(NCH):
                    bk, Bbk = next_bank()

                    def fn(pe, bk=bk, c=c):
                        last = None
                        for tb in range(4):
                            last = pe.transpose(bk[:, tb * 128:(tb + 1) * 128],
                                                ystage[:, tb, c * 128:(c + 1) * 128], ident_f)
                        return [last]
                    P.op("pe", fn, reads=[B["y"], B["cf"]], writes=[Bbk])
                    cp("dve" if c % 2 == 0 else "act", xT[:, c, :], bk[:], [Bbk], [B["xT"]])
            else:
                dma("pool", xT_t[:], x1T_d[s, g], [Bx1[s][g]], [B["xT"]], B["xT"])

            prenorm(b, 0)

            if STOP < 4:
                return drain_tiles()
            if g == 0:
                mset("dve", uT_t[:, :, 0:HALO], 0.0, [B["uT"]])
            else:
                cp("dve", uT_t[:, :, 0:HALO], uT_t[:, :, T:T + HALO], [B["uT"]], [B["uT"]])
            for c in range(4):
                wt, Bwt = get_tile(l, c)
                bv, Bbv = next_bank()
                bg, Bbg = next_bank()
                mm(bv[:], [(wt[:, kc * 256:kc * 256 + 128], hT_t[:, kc, :]) for kc in range(NCH)],
                   [Bwt, B["hT"]], Bbv)
                mm(bg[:], [(wt[:, kc * 256 + 128:kc * 256 + 256], hT_t[:, kc, :]) for kc in range(NCH)],
                   [Bwt, B["hT"]], Bbg)
                done_tile()
                i = rot("sig", 2)
                act(sig_t[i][:], bg[:], AF.Sigmoid, [Bbg], [Bsig[i]])
                tt("dve", uT_t[:, c, HALO:HALO + T], bv[:], sig_t[i][:], ALU.mult, [Bbv, Bsig[i]], [B["uT"]])

            conv_taps = [(c, j) for c in range(4) for j in range(KCONV)]

            def emit_tap(c, j):
                wcol = spm_t[:, SP_CONVW + j * 4 + c:SP_CONVW + j * 4 + c + 1]
                if j == 0:
                    ts("dve", vconv[:, c, :], uT_t[:, c, 0:T], wcol,
                       spm_t[:, SP_CONVB + c:SP_CONVB + c + 1], ALU.mult, ALU.add,
                       [B["uT"], B["spm"]], [B["y"]])
                else:
                    k = rot("ptmp", 2)
                    ts("dve", ptmp_t[k][:], uT_t[:, c, j:j + T], wcol, None, ALU.mult, None,
                       [B["uT"], B["spm"]], [Bptmp[k]])
                    tt("pool", vconv[:, c, :], vconv[:, c, :], ptmp_t[k][:], ALU.add,
                       [B["y"], Bptmp[k]], [B["y"]])

            if STOP < 5:
                return drain_tiles()
            wt, Bwt = get_tile(l, 4)
            for ch in range(4):
                bk, Bbk = next_bank()
                mm(bk[:], [(wt[:, kc * 512 + ch * 128:kc * 512 + (ch + 1) * 128], hT_t[:, kc, :])
                           for kc in range(NCH)], [Bwt, B["hT"]], Bbk)
                act(qT[:, ch, :], bk[:], AF.Copy, [Bbk], [B["R1"]], scale=0.125)
            done_tile()
            wt, Bwt = get_tile(l, 5)
            for ch in range(4):
                bk, Bbk = next_bank()
                mm(bk[:], [(wt[:, kc * 512 + ch * 128:kc * 512 + (ch + 1) * 128], hT_t[:, kc, :])
                           for kc in range(NCH)], [Bwt, B["hT"]], Bbk)
                cp("dve", kT_t[:, ch, tok0:tok0 + T], bk[:], [Bbk], [B["kT"]])
            done_tile()
            wt, Bwt = get_tile(l, 6)
            for tb in range(4):
                bk, Bbk = next_bank()
                mm(bk[:], [(hT_t[:, kc, tb * 128:(tb + 1) * 128], wt[:, kc * 512:(kc + 1) * 512])
                           for kc in range(NCH)], [Bwt, B["hT"]], Bbk)
                cp("dve" if tb % 2 == 0 else "act", V_t[:, 4 * g + tb, :], bk[:], [Bbk], [B["V"]])
            done_tile()

            zb = [(bank_t[0], Bbank[0]), (bank_t[1], Bbank[1])]
            accb = [(bank_t[2], Bbank[2]), (bank_t[3], Bbank[3])]
            ob = [(bank_t[4], Bbank[4]), (bank_t[5], Bbank[5])]
            items = []
            for ch in range(4):
                kbs = list(range(4 * g + 3, -1, -1))
                for ki, kb in enumerate(kbs):
                    for hh in range(2):
                        items.append((ch, hh, kb, ki == 0, ki == len(kbs) - 1))
            n_it = len(items)

            def geom(it):
                ch, hh, kb, first, lastk = it
                r = kb - 4 * g
                q0 = 128 * r if r >= 0 else 0
                return ch, hh, kb, first, lastk, r, q0, hh * 64

            def st0(i):
                ch, hh, kb, first, lastk, r, q0, pb = geom(items[i])
                z, Bz = zb[i % 2]
                mm(z[:, q0:T], [(kT_t[pb:pb + 64, ch, kb * 128:(kb + 1) * 128], qT[pb:pb + 64, ch, q0:T])],
                   [B["kT"], B["R1"]], Bz)

            def st1(i):
                ch, hh, kb, first, lastk, r, q0, pb = geom(items[i])
                z, Bz = zb[i % 2]
                e, Bei = e_t[i % 4], Be[i % 4]
                act(e[:, q0:T], z[:, q0:T], AF.Exp, [Bz], [Bei])
                if r >= 0:
                    tt("dve", e[:, q0:q0 + 128], e[:, q0:q0 + 128], mask_f, ALU.mult, [Bei, B["cf"]], [Bei])
                act(sp_t[i % 3][:, q0:T], e[:, q0:T], AF.Ln, [Bei], [Bsp[i % 3]], bias=1.0)

            def st2(i):
                ch, hh, kb, first, lastk, r, q0, pb = geom(items[i])
                acc, Bacc = accb[hh]
                mm(acc[:, q0:T], [(tri1_bf, sp_t[i % 3][:, q0:T])], [B["cbf"], Bsp[i % 3]], Bacc,
                   start=first, stop=False, skip=True)

            def st3(i):
                ch, hh, kb, first, lastk, r, q0, pb = geom(items[i])
                acc, Bacc = accb[hh]
                e, Bei = e_t[i % 4], Be[i % 4]
                ea, Bei2 = ea_t[i % 2], Bea[i % 2]
                act(ea[:, q0:T], acc[:, q0:T], AF.Exp, [Bacc], [Bei2])
                if not lastk:
                    mm(acc[:, q0:T], [(tri2_bf, sp_t[i % 3][:, q0:T])], [B["cbf"], Bsp[i % 3]], Bacc,
                       start=False, stop=False, skip=True)
                tt("dve", A_t[i % 2][:, q0:T], e[:, q0:T], ea[:, q0:T], ALU.mult, [Bei, Bei2], [BA[i % 2]])
                o, Bo = ob[ch % 2]
                mm(o[pb:pb + 64, q0:T], [(V_t[:, kb, (2 * ch + hh) * 64:(2 * ch + hh + 1) * 64],
                                          A_t[i % 2][:, q0:T])],
                   [B["V"], BA[i % 2]], Bo, start=first, stop=lastk, skip=True)
                if lastk and hh == 1:
                    cp("dve", oT[:, ch, :], o[:], [Bo], [B["R1"]])

            tap_i = 0
            for step in range(n_it + 3):
                ntap = -(-(len(conv_taps) - tap_i) // (n_it + 3 - step))
                for _ in range(ntap):
                    emit_tap(*conv_taps[tap_i])
                    tap_i += 1
                if 0 <= step - 3 < n_it:
                    st3(step - 3)
                if 0 <= step - 2 < n_it:
                    st2(step - 2)
                if 0 <= step - 1 < n_it:
                    st1(step - 1)
                if step < n_it:
                    st0(step)

            if STOP < 6:
                return drain_tiles()
            bk, Bbk = next_bank()
            mm(bk[:], [(mean_f, vconv[:, c, :]) for c in range(4)], [B["cf"], B["y"]], Bbk)
            if STOP < 6.05:
                return drain_tiles()
            for c in range(4):
                tt("dve", vconv[:, 4 + c, :], vconv[:, c, :], bk[:], ALU.subtract, [B["y"], Bbk], [B["y"]])
            for c in range(4):
                act(vconv[:, c, :], vconv[:, 4 + c, :], AF.Square, [B["y"]], [B["y"]])
            if STOP < 6.15:
                return drain_tiles()
            bk2, Bbk2 = next_bank()
            mm(bk2[:], [(mean_f, vconv[:, c, :]) for c in range(4)], [B["cf"], B["y"]], Bbk2)
            rstd_from(bk2, Bbk2, 1.0)
            if STOP < 6.25:
                return drain_tiles()
            for c in range(4):
                i = rot("tmp", 3)
                tt("dve", tmp_t[i][:], vconv[:, 4 + c, :], rstd_t[:], ALU.mult, [B["y"], B["rstd"]], [Btmp[i]])
                gcol = spm_t[:, SP_LNG + c:SP_LNG + c + 1]
                bcol = spm_t[:, SP_LNB + c:SP_LNB + c + 1]
                j = rot("sig", 2)
                act(sig_t[j][:], tmp_t[i][:], AF.Sigmoid, [Btmp[i], B["spm"]], [Bsig[j]], bias=bcol, scale=gcol)
                if STOP < 6.28:
                    continue
                ts("dve", tmp_t[i][:], tmp_t[i][:], gcol, bcol, ALU.mult, ALU.add,
                   [Btmp[i], B["spm"]], [Btmp[i]])
                tt("dve", sT[:, c, :], tmp_t[i][:], sig_t[j][:], ALU.mult, [Btmp[i], Bsig[j]], [B["R1"]])
            if STOP < 6.35:
                return drain_tiles()

            for c in range(8):
                wt, Bwt = get_tile(l, 7 + c)
                byc, Bbyc = next_bank()
                bya, Bbya = next_bank()
                bgc, Bbgc = next_bank()
                bga, Bbga = next_bank()
                mm(byc[:], [(wt[:, kc * 128:(kc + 1) * 128], sT[:, kc, :]) for kc in range(4)],
                   [Bwt, B["R1"]], Bbyc)
                mm(bya[:], [(wt[:, (4 + kc) * 128:(5 + kc) * 128], oT[:, kc, :]) for kc in range(4)],
                   [Bwt, B["R1"]], Bbya)
                mm(bgc[:], [(wt[:, (8 + kc) * 128:(9 + kc) * 128], hT_t[:, kc, :]) for kc in range(NCH)],
                   [Bwt, B["hT"]], Bbgc)
                mm(bga[:], [(wt[:, (16 + kc) * 128:(17 + kc) * 128], hT_t[:, kc, :]) for kc in range(NCH)],
                   [Bwt, B["hT"]], Bbga)
                done_tile()
                act(sig_t[0][:], bgc[:], AF.Sigmoid, [Bbgc], [Bsig[0]])
                act(sig_t[1][:], bga[:], AF.Sigmoid, [Bbga], [Bsig[1]])
                i = rot("tmp", 3)
                i2 = rot("tmp", 3)
                tt("dve", tmp_t[i][:], byc[:], sig_t[0][:], ALU.mult, [Bbyc, Bsig[0]], [Btmp[i]])
                tt("dve", tmp_t[i2][:], bya[:], sig_t[1][:], ALU.mult, [Bbya, Bsig[1]], [Btmp[i2]])
                tt("pool", mT[:, c, :], tmp_t[i][:], tmp_t[i2][:], ALU.add, [Btmp[i], Btmp[i2]], [B["R1"]])

            if STOP < 6.45:
                return drain_tiles()
            wo_state = {}

            def produce_wo(c):
                if c % 4 == 0:
                    if c > 0:
                        done_tile()
                    wo_state["t"] = get_tile(l, 15 + c // 4)
                wt, Bwt = wo_state["t"]
                bk, Bbk = next_bank()
                cc = c % 4
                mm(bk[:], [(wt[:, kc * 512 + cc * 128:kc * 512 + (cc + 1) * 128], mT[:, kc, :])
                           for kc in range(NCH)], [Bwt, B["R1"]], Bbk)
                return bk, Bbk
            postnorm_residual(b, 0, produce_wo)
            done_tile()

            if STOP < 7:
                return drain_tiles()
            prenorm(b, 1)
            for i in range(11):
                wt, Bwt = get_tile(l, 17 + i)
                bks = [next_bank() for _ in range(4)]
                for w in range(2):
                    for jj in range(2):
                        bk, Bbk = bks[w * 2 + jj]
                        mm(bk[:], [(wt[:, kc * 512 + w * 256 + jj * 128:kc * 512 + w * 256 + (jj + 1) * 128],
                                    hT_t[:, kc, :]) for kc in range(NCH)], [Bwt, B["hT"]], Bbk)
                done_tile()
                for jj in range(2):
                    gk, Bgk = bks[jj]
                    uk, Buk = bks[2 + jj]
                    si = rot("sig", 2)
                    ti_ = rot("tmp", 3)
                    act(sig_t[si][:], gk[:], AF.Sigmoid, [Bgk], [Bsig[si]])
                    tt("dve", tmp_t[ti_][:], gk[:], sig_t[si][:], ALU.mult, [Bgk, Bsig[si]], [Btmp[ti_]])
                    tt("dve", hid[:, 2 * i + jj, :], uk[:], tmp_t[ti_][:], ALU.mult, [Buk, Btmp[ti_]], [B["R1"]])

            def produce_fout(c):
                wt, Bwt = get_tile(l, 28 + c)
                bk, Bbk = next_bank()
                mm(bk[:], [(wt[:, kc * 128:(kc + 1) * 128], hid[:, kc, :]) for kc in range(NHID)],
                   [Bwt, B["R1"]], Bbk)
                done_tile()
                return bk, Bbk
            postnorm_residual(b, 1, produce_fout)

            if not last_layer:
                dma("pool", x1T_d[s, g], xT_t[:], [B["xT"]], [Bx1[s][g]], B["xT"])
            else:
                for tb in range(4):
                    for half in range(2):
                        bk, Bbk = next_bank()

                        def fn(pe, bk=bk, tb=tb, half=half):
                            last = None
                            for cc in range(4):
                                last = pe.transpose(bk[:, cc * 128:(cc + 1) * 128],
                                                    xT[:, half * 4 + cc, tb * 128:(tb + 1) * 128], ident_f)
                            return [last]
                        P.op("pe", fn, reads=[B["xT"], B["cf"]], writes=[Bbk])
                        cp("dve" if half == 0 else "act", ystage[:, tb, half * 512:(half + 1) * 512], bk[:],
                           [Bbk], [B["y"]])
                dma("pool", out_d[s, tok0:tok0 + T, :].rearrange("(tb p) d -> p tb d", p=128), ystage,
                    [B["y"]], [B["out"]], B["y"])

        for l in range(DEPTH):
            layer_setup(l)
            for s in range(NSEQ):
                for g in range(NG):
                    group(l, s, g)
        if STOP < 99:
            dma("pool", out_d[0, 0:128, :], y_t[:, 0:D], [B["y"]], [B["out"]], B["y"])
        P.final_wait("pool", [B["y"], B["out"]])

        with nc.Block() as block:
            @block.tensor
            def _(eng):
                P.emit("pe", eng)

            @block.scalar
            def _(eng):
                P.emit("act", eng)

            @block.vector
            def _(eng):
                P.emit("dve", eng)

            @block.gpsimd
            def _(eng):
                P.emit("pool", eng)

            @block.sync
            def _(eng):
                P.emit("sp", eng)
    return nc


def make_consts():
    cf = np.zeros((128, NCF), np.float32)
    idx = np.arange(128)
    cf[:, CF_IDENT:CF_IDENT + 128] = np.eye(128, dtype=np.float32)
    cf[:, CF_ONES:CF_ONES + 128] = 1.0
    cf[:, CF_MEAN:CF_MEAN + 128] = 1.0 / CW
    cf[:, CF_MASK:CF_MASK + 128] = (idx[:, None] < idx[None, :]).astype(np.float32)
    cf[:, CF_TRI1:CF_TRI1 + 128] = -(idx[:, None] >= idx[None, :]).astype(np.float32)
    cf[:, CF_TRI2:CF_TRI2 + 128] = -(idx[:, None] < idx[None, :]).astype(np.float32)
    cf[:, CF_EPS:CF_EPS + 4] = EPS
    return cf


def make_smallp(depth, pre_mix_g, post_mix_g, pre_ffn_g, post_ffn_g, ada_b, conv_b, conv_ln_g,
                conv_ln_b, conv_w):
    sp = np.zeros((depth, 128, NSP), np.float32)

    def fm(v):
        return np.ascontiguousarray(v.reshape(-1, 128).T)
    for l in range(depth):
        sp[l, :, SP_PRE1:SP_PRE1 + 8] = fm(pre_mix_g[l])
        sp[l, :, SP_POST1:SP_POST1 + 8] = fm(post_mix_g[l])
        sp[l, :, SP_PRE2:SP_PRE2 + 8] = fm(pre_ffn_g[l])
        sp[l, :, SP_POST2:SP_POST2 + 8] = fm(post_ffn_g[l])
        sp[l, :, SP_ADAB:SP_ADAB + 48] = fm(ada_b[l])
        sp[l, :, SP_CONVB:SP_CONVB + 4] = fm(conv_b[l])
        sp[l, :, SP_LNG:SP_LNG + 4] = fm(conv_ln_g[l])
        sp[l, :, SP_LNB:SP_LNB + 4] = fm(conv_ln_b[l])
        cw = conv_w[l].reshape(KCONV, 4, 128).transpose(2, 0, 1)
        sp[l, :, SP_CONVW:SP_CONVW + KCONV * 4] = cw.reshape(128, KCONV * 4)
    return sp


def run(inputs, nseq, S, depth, ncores, trace=False, stop=99):
    x = np.asarray(inputs["x"], np.float32)
    c = np.asarray(inputs["c"], np.float32)
    f = lambda k: np.ascontiguousarray(np.asarray(inputs[k], np.float32))
    smallp = make_smallp(depth, f("pre_mix_g"), f("post_mix_g"), f("pre_ffn_g"), f("post_ffn_g"),
                         f("ada_b"), f("conv_b"), f("conv_ln_g"), f("conv_ln_b"), f("conv_w"))
    consts = make_consts()
    shared = {"ada_w": f("ada_w")[:depth], "smallp": smallp, "consts": consts, "w_in": f("w_in")[:depth],
              "w_conv_out": f("w_conv_out")[:depth], "w_att_out": f("w_att_out")[:depth],
              "w_o": f("w_o")[:depth], "w_ffn_in": f("w_ffn_in")[:depth],
              "w_ffn_out": f("w_ffn_out")[:depth]}
    in_maps = []
    for k in range(ncores):
        xs = np.ascontiguousarray(x[k * nseq:(k + 1) * nseq])
        cs = c[k * nseq:(k + 1) * nseq]
        cT = np.ascontiguousarray(cs.reshape(nseq, NCH, 128).transpose(2, 1, 0))
        m = dict(shared)
        m["x"] = xs
        m["cT"] = cT
        in_maps.append(m)
    nc = build(nseq, S, depth, stop)
    res = run_bass_kernel_spmd(nc, in_maps, core_ids=list(range(ncores)), trace=trace)
    out = np.concatenate([np.asarray(r["out"]) for r in res.results], axis=0)
    return out.astype(np.float32), res


def kernel(**inputs):
    out, _ = run(inputs, 2, 4096, 2, NCORES)
    return out
```
